# Optimizing a Trainium2 kernel written in Bass

```python
import jax, jax.numpy as jnp
from jax import lax
import numpy as np

D_MODEL = 1024
BATCH = 2
SEQ = 8192
DEPTH = 4

N_MIXERS = 3
N_A = (DEPTH + 2) // 3
N_B = (DEPTH + 1) // 3
N_C = DEPTH // 3
D_RNN = 1280
A_HEADS = 16
A_HEAD_DIM = D_RNN // A_HEADS
A_CONV = 4
LRU_C = 8.0
D_SGU = D_MODEL
SGU_CHUNK = 128
SGU_GROUPS = 8
SGU_GROUP_DIM = D_SGU // SGU_GROUPS
D_CONV = D_MODEL
C_CONV = 3
D_FF = 4 * D_MODEL
EPS = 1e-6

kernel_name = "hybrid_rglru_sgu_shortconv_trunk"


def _rmsnorm(x, g):
    x32 = x.astype(jnp.float32)
    y = x32 * lax.rsqrt(jnp.mean(x32 * x32, axis=-1, keepdims=True) + EPS)
    return y.astype(x.dtype) * g


def _causal_dwconv(x, w):
    k = w.shape[0]
    return lax.conv_general_dilated(
        x, w[:, None, :].astype(x.dtype), window_strides=(1,),
        padding=[(k - 1, 0)], dimension_numbers=("NWC", "WIO", "NWC"),
        feature_group_count=x.shape[-1])


def _lru_combine(left, right):
    a_l, b_l = left
    a_r, b_r = right
    return a_l * a_r, a_r * b_l + b_r


def _rglru_mixer(h, w_in, conv_w, conv_b, gate_a_w, gate_a_b, gate_x_w, gate_x_b, lam, w_out):
    b, s, _ = h.shape
    gate, xr = jnp.split(h @ w_in, 2, axis=-1)
    gate = jax.nn.gelu(gate)
    xr = _causal_dwconv(xr, conv_w) + conv_b
    xh = xr.reshape(b, s, A_HEADS, A_HEAD_DIM)
    r = jax.nn.sigmoid(jnp.einsum("bshi,hij->bshj", xh, gate_a_w).reshape(b, s, D_RNN) + gate_a_b)
    ig = jax.nn.sigmoid(jnp.einsum("bshi,hij->bshj", xh, gate_x_w).reshape(b, s, D_RNN) + gate_x_b)
    log_a = -LRU_C * r.astype(jnp.float32) * jax.nn.softplus(-lam.astype(jnp.float32))
    a = jnp.exp(log_a)
    u = jnp.sqrt(-jnp.expm1(2.0 * log_a)) * (ig * xr).astype(jnp.float32)
    _, hs = lax.associative_scan(_lru_combine, (a, u), axis=1)
    return (hs.astype(h.dtype) * gate) @ w_out


def _sgu_mixer(h, w_in, norm_g, w_s, s_bias, w_out):
    b, s, _ = h.shape
    z = jax.nn.gelu(h @ w_in)
    u, v = jnp.split(z, 2, axis=-1)
    v = _rmsnorm(v, norm_g)
    n_chunks = s // SGU_CHUNK
    v = v.reshape(b, n_chunks, SGU_CHUNK, SGU_GROUPS, SGU_GROUP_DIM)
    causal = jnp.tril(jnp.ones((SGU_CHUNK, SGU_CHUNK), dtype=bool))
    w_causal = jnp.where(causal[None], w_s, jnp.zeros((), w_s.dtype))
    mixed = jnp.einsum("gts,bnsgc->bntgc", w_causal, v) + s_bias.T[None, None, :, :, None]
    y = u * mixed.reshape(b, s, D_SGU)
    return y @ w_out


def _shortconv_mixer(h, w_in, conv_w, w_out):
    gb, gc, xv = jnp.split(h @ w_in, 3, axis=-1)
    y = gb * _causal_dwconv(gc * xv, conv_w)
    return y @ w_out


def _sqrelu_mlp(h, w1, w2):
    return jnp.square(jax.nn.relu(h @ w1)) @ w2


def setup_inputs(seed: int = 0) -> dict:
    key = jax.random.key(seed)
    ks = jax.random.split(key, 24)
    f32 = jnp.float32

    def w(k, shape, fan_in):
        return jax.random.normal(k, shape, f32) * (fan_in ** -0.5)

    def gain(k, shape):
        return 1.0 + 0.1 * jax.random.normal(k, shape, f32)

    def bias(k, shape, scale=0.1):
        return scale * jax.random.normal(k, shape, f32)

    a_c = jax.random.uniform(ks[11], (N_A, D_RNN), f32, minval=0.9, maxval=0.999)
    a0 = a_c ** (1.0 / LRU_C)
    a_lambda = jnp.log(a0) - jnp.log1p(-a0)

    return {
        "x": jax.random.normal(ks[0], (BATCH, SEQ, D_MODEL), f32),
        "norm_mix_g": gain(ks[1], (DEPTH, D_MODEL)),
        "norm_mlp_g": gain(ks[2], (DEPTH, D_MODEL)),
        "final_norm_g": gain(ks[3], (D_MODEL,)),
        "a_w_in": w(ks[4], (N_A, D_MODEL, 2 * D_RNN), D_MODEL),
        "a_conv_w": w(ks[5], (N_A, A_CONV, D_RNN), A_CONV),
        "a_conv_b": bias(ks[6], (N_A, D_RNN)),
        "a_gate_a_w": w(ks[7], (N_A, A_HEADS, A_HEAD_DIM, A_HEAD_DIM), A_HEAD_DIM),
        "a_gate_a_b": bias(ks[8], (N_A, D_RNN)),
        "a_gate_x_w": w(ks[9], (N_A, A_HEADS, A_HEAD_DIM, A_HEAD_DIM), A_HEAD_DIM),
        "a_gate_x_b": bias(ks[10], (N_A, D_RNN)),
        "a_lambda": a_lambda,
        "a_w_out": w(ks[12], (N_A, D_RNN, D_MODEL), D_RNN),
        "b_w_in": w(ks[13], (N_B, D_MODEL, 2 * D_SGU), D_MODEL),
        "b_norm_g": gain(ks[14], (N_B, D_SGU)),
        "b_w_s": w(ks[15], (N_B, SGU_GROUPS, SGU_CHUNK, SGU_CHUNK), SGU_CHUNK),
        "b_s_bias": gain(ks[16], (N_B, SGU_GROUPS, SGU_CHUNK)),
        "b_w_out": w(ks[17], (N_B, D_SGU, D_MODEL), D_SGU),
        "c_w_in": w(ks[18], (N_C, D_MODEL, 3 * D_CONV), D_MODEL),
        "c_conv_w": w(ks[19], (N_C, C_CONV, D_CONV), C_CONV),
        "c_w_out": w(ks[20], (N_C, D_CONV, D_MODEL), D_CONV),
        "mlp_w1": w(ks[21], (DEPTH, D_MODEL, D_FF), D_MODEL),
        "mlp_w2": w(ks[22], (DEPTH, D_FF, D_MODEL), D_FF),
    }


def reference(x, norm_mix_g, norm_mlp_g, final_norm_g,
              a_w_in, a_conv_w, a_conv_b, a_gate_a_w, a_gate_a_b, a_gate_x_w, a_gate_x_b, a_lambda, a_w_out,
              b_w_in, b_norm_g, b_w_s, b_s_bias, b_w_out,
              c_w_in, c_conv_w, c_w_out,
              mlp_w1, mlp_w2):
    for i in range(DEPTH):
        kind, j = i % N_MIXERS, i // N_MIXERS
        h = _rmsnorm(x, norm_mix_g[i])
        if kind == 0:
            mix = _rglru_mixer(h, a_w_in[j], a_conv_w[j], a_conv_b[j], a_gate_a_w[j], a_gate_a_b[j],
                               a_gate_x_w[j], a_gate_x_b[j], a_lambda[j], a_w_out[j])
        elif kind == 1:
            mix = _sgu_mixer(h, b_w_in[j], b_norm_g[j], b_w_s[j], b_s_bias[j], b_w_out[j])
        else:
            mix = _shortconv_mixer(h, c_w_in[j], c_conv_w[j], c_w_out[j])
        x = x + mix
        x = x + _sqrelu_mlp(_rmsnorm(x, norm_mlp_g[i]), mlp_w1[i], mlp_w2[i])
    return _rmsnorm(x, final_norm_g)
```

```python
import numpy as np
import concourse.bass as bass
import concourse.mybir as mybir
from concourse.bass_utils import run_bass_kernel_spmd

F32 = mybir.dt.float32
BF16 = mybir.dt.bfloat16
AF = mybir.ActivationFunctionType
ALU = mybir.AluOpType

NCORES = 8
D = 1024
DC = 8
SEQ = 8192
TOK = 2048
TP = 1024
NT = 512
DR = 1280
RC = 10
SL = 2048
NSLOT = 5
EPS = 1e-6
N_LAYERS = 4


class Sched:
    def __init__(self):
        self.ops = []
        self.last_w = {}
        self.readers = {}
        self.barrier_deps = set()

    def add(self, q, fn, reads=(), writes=(), dma=None, nobarrier=False):
        oid = len(self.ops)
        deps = set()
        for r in reads:
            w = self.last_w.get(r)
            if w is not None:
                deps.add(w)
        for r in writes:
            w = self.last_w.get(r)
            if w is not None:
                deps.add(w)
            deps.update(self.readers.get(r, ()))
        if not nobarrier:
            deps.update(self.barrier_deps)
        for r in reads:
            self.readers.setdefault(r, set()).add(oid)
        for r in writes:
            self.last_w[r] = oid
            self.readers[r] = set()
        deps.discard(oid)
        self.ops.append(dict(id=oid, q=q, fn=fn, dma=dma, deps=deps))
        return oid

    def barrier(self, exclude=()):
        last = {}
        for op in self.ops:
            if op["id"] in exclude:
                continue
            key = ("dma", op["dma"]) if op["dma"] else ("q", op["q"])
            last[key] = op["id"]
        self.barrier_deps = set(last.values())

    def emit(self, nc, engines, final_waits):
        ops = self.ops
        for op in ops:
            if op["q"] == "pe" and not op["dma"]:
                op["deps"] = {d for d in op["deps"] if not (ops[d]["q"] == "pe" and not ops[d]["dma"])}
        signalled = set()
        for op in ops:
            signalled.update(op["deps"])
        for oid in final_waits:
            signalled.add(oid)
        cnt = {}
        for op in ops:
            if op["id"] in signalled:
                key = ("dma", op["dma"]) if op["dma"] else ("q", op["q"])
                cnt[key] = cnt.get(key, 0) + (16 if op["dma"] else 1)
                op["sig"] = (key, cnt[key])
        keys = sorted(cnt.keys(), key=str)
        sems = {}
        import contextlib
        with contextlib.ExitStack() as es:
            for i, k in enumerate(keys):
                sems[k] = es.enter_context(nc.semaphore("s%d" % i))
            block = es.enter_context(nc.Block())
            byq = {}
            for op in ops:
                byq.setdefault(op["q"], []).append(op)

            def body(qname):
                def f(eng):
                    waited = {}
                    for op in byq.get(qname, []):
                        need = {}
                        for d in op["deps"]:
                            k, v = ops[d]["sig"]
                            if v > need.get(k, 0):
                                need[k] = v
                        for k, v in need.items():
                            if waited.get(k, 0) < v:
                                eng.wait_ge(sems[k], v)
                                waited[k] = v
                        ins = op["fn"](eng)
                        if "sig" in op:
                            k, v = op["sig"]
                            ins.then_inc(sems[k], 16 if op["dma"] else 1)
                    if qname == "sp":
                        for oid in final_waits:
                            k, v = ops[oid]["sig"]
                            eng.wait_ge(sems[k], v)
                return f

            for qname, deco in engines.items():
                getattr(block, deco)(body(qname))


def _k8(W, cols):
    assert len(cols) % 2 == 0
    out = np.zeros((len(cols) // 2, 128, SL), np.float32)
    Wk = W.reshape(8, 128, W.shape[1])
    for i, c0 in enumerate(cols):
        blk = Wk[:, :, c0:c0 + 128]
        view = out[i // 2].reshape(128, 8, 256)
        view[:, :, (i % 2) * 128:(i % 2) * 128 + 128] = blk.transpose(1, 0, 2)
    return out


def _w2(W2):
    out = np.zeros((16, 128, SL), np.float32)
    Wk = W2.reshape(8, 4, 128, 1024)
    for jg in range(8):
        for half in range(2):
            out[jg * 2 + half].reshape(128, 4, 512)[:] = Wk[jg, :, :, half * 512:(half + 1) * 512].transpose(1, 0, 2)
    return out


def _aout(W):
    out = np.zeros((8, 128, SL), np.float32)
    Wk = W.reshape(10, 128, 1024)
    for m in range(8):
        out[m, :, :1280].reshape(128, 10, 128)[:] = Wk[:, :, m * 128:(m + 1) * 128].transpose(1, 0, 2)
    return out


def _gate_plan():
    K = []
    for mc in range(RC):
        h0, h1 = (mc * 128) // 80, (mc * 128 + 127) // 80
        lo, hi = h0 * 80, h1 * 80 + 79
        K.append(list(range(lo // 128, hi // 128 + 1)))
    groups = [[0, 1, 2], [3, 4], [5, 6, 7], [8, 9]]
    plan = {}
    for gi, grp in enumerate(groups):
        t = 0
        for mc in grp:
            for gate in range(2):
                plan[(mc, gate)] = (gi, [(kc, t + i) for i, kc in enumerate(K[mc])])
                t += len(K[mc])
        assert t <= 16
    return K, groups, plan


def _gates(Ga, Gx):
    K, groups, plan = _gate_plan()
    bd = []
    for G in (Ga, Gx):
        M = np.zeros((DR, DR), np.float32)
        for h in range(16):
            M[h * 80:(h + 1) * 80, h * 80:(h + 1) * 80] = G[h]
        bd.append(M)
    out = np.zeros((4, 128, SL), np.float32)
    for (mc, gate), (gi, tiles) in plan.items():
        for kc, t in tiles:
            out[gi, :, t * 128:(t + 1) * 128] = bd[gate][kc * 128:(kc + 1) * 128, mc * 128:(mc + 1) * 128]
    return out


def _vec(v, n):
    return np.ascontiguousarray(np.asarray(v, np.float32).reshape(n, 128).T)


def build(nl=N_LAYERS, final_norm=True):
    nc = bass.Bass("TRN2", target_bir_lowering=False)
    S = Sched()

    cv = {}
    ncv = [0]

    def cvalloc(name, n):
        cv[name] = ncv[0]
        ncv[0] += n
    for i in range(4):
        cvalloc("gmix%d" % i, 8)
        cvalloc("gmlp%d" % i, 8)
    cvalloc("gfin", 8)
    for j in range(2):
        for k in range(4):
            cvalloc("acw%d_%d" % (j, k), 10)
        for nm in ("acb", "agab", "agxb", "alam"):
            cvalloc("%s%d" % (nm, j), 10)
    cvalloc("bng", 8)
    for k in range(3):
        cvalloc("ccw%d" % k, 8)
    for nm in ("mP", "mQ", "mC", "mS"):
        cvalloc(nm, 8)
    NCV = ncv[0]

    wi = {}
    nw = [0]

    def walloc(name, n):
        wi[name] = nw[0]
        nw[0] += n
    for j in range(2):
        walloc("a_x%d" % j, 5)
        walloc("a_g%d" % j, 5)
        walloc("a_gt%d" % j, 4)
        walloc("a_o%d" % j, 8)
    walloc("b_v", 4)
    walloc("b_u", 4)
    walloc("b_o", 4)
    walloc("c_i", 12)
    walloc("c_o", 4)
    for i in range(4):
        walloc("m1_%d" % i, 16)
        walloc("m2_%d" % i, 16)
    NSL = nw[0]

    xT_d = nc.dram_tensor("xT", [128, 4, DC, NT], F32, kind="ExternalInput").ap()
    xh_d = nc.dram_tensor("xhalo", [128, 2 * DC * 3], F32, kind="ExternalInput").ap()
    cv_d = nc.dram_tensor("cvec", [128, NCV], F32, kind="ExternalInput").ap()
    wall_d = nc.dram_tensor("wall", [NSL, 128, SL], F32, kind="ExternalInput").ap()
    wsT_d = nc.dram_tensor("wsT", [128, 8 * 128], F32, kind="ExternalInput").ap()
    tri_d = nc.dram_tensor("tri", [128, 128], F32, kind="ExternalInput").ap()
    sb_d = nc.dram_tensor("sbias", [128, 8 * 128], F32, kind="ExternalInput").ap()
    id_d = nc.dram_tensor("ident", [128, 128], F32, kind="ExternalInput").ap()
    out_d = nc.dram_tensor("outT", [128, 4, DC, NT], F32, kind="ExternalOutput").ap()
    NEX = 7
    XCOLS = [30, 30, 48, 48, 30, 30, 30]
    ib_d = [nc.dram_tensor("ib%d" % i, [128, XCOLS[i]], F32) for i in range(NEX)]
    ob_d = [nc.dram_tensor("ob%d" % i, [NCORES * 128, XCOLS[i]], F32) for i in range(NEX)]

    import contextlib
    es = contextlib.ExitStack()
    sb = lambda name, shape, dt: es.enter_context(nc.sbuf_tensor(name, shape, dt))
    xT = sb("xTs", [128, DC, TOK], F32)
    slots = sb("slots", [128, NSLOT, SL], BF16)
    AW = 29744
    arena = sb("arena", [128, AW], F32)
    cvt = sb("cvt", [128, NCV], F32)
    ones = sb("ones", [128, 128], BF16)
    idb = sb("idb", [128, 128], BF16)
    small = sb("small", [128, 128], F32)
    wsmt = sb("wsm", [128, 1024], BF16)
    xh = sb("xh", [128, 2 * DC * 3], F32)
    pay = sb("pay", [128, 48], F32)
    gath = sb("gath", [128, NCORES * 48], F32)
    stash = sb("stash", [128, 10], F32)
    ps = [es.enter_context(nc.psum_tensor("ps%d" % i, [128, 512], F32)) for i in range(8)]

    SC = dict(sc=0, sc2=10, e=20, hba=30, Hc=40, t1=50, t2=60, hbx=70, rsh=80, rstdh=84, tmp3=88)

    def carve(off, nwords, dt, pat=None, **kw):
        v = arena[:, off:off + nwords]
        if dt == BF16:
            v = v.bitcast(BF16)
        if pat:
            v = v.rearrange(pat, **kw)
        return v

    def _mk(meth, kw):
        return lambda e: getattr(e, meth)(**kw)

    def act(reads, writes, **kw):
        S.add("act", _mk("activation", kw), reads, writes)

    def dve(meth, reads, writes, **kw):
        S.add("dve", _mk(meth, kw), reads, writes)

    def pe(reads, writes, **kw):
        S.add("pe", _mk("matmul", kw), reads, writes)

    def pet(reads, writes, **kw):
        S.add("pe", _mk("transpose", kw), reads, writes)

    dmactr = [0]

    def dma(q, key, out, in_, reads, writes, nobarrier=False):
        if key is None:
            key = "u%d" % dmactr[0]
            dmactr[0] += 1
        return S.add(q, _mk("dma_start", dict(out=out, in_=in_)), reads, writes, dma=key, nobarrier=nobarrier)

    mmctr = [0]
    mmpool = [[0, 1, 2, 3, 4, 7]]

    def mmbank():
        b = mmpool[0][mmctr[0] % len(mmpool[0])]
        mmctr[0] += 1
        return b

    slotctr = [0]

    def wget(idx, width=SL):
        s = slotctr[0] % NSLOT
        slotctr[0] += 1
        dma("pool", "slot%d" % s, slots[:, s, 0:width], wall_d[idx, :, 0:width], [], ["slot%d" % s], nobarrier=True)
        return s

    def cvc(name, c):
        return cvt[:, cv[name] + c:cv[name] + c + 1]

    def sm(name, n=10, off=0):
        return small[:, SC[name] + off:SC[name] + off + n]

    def xres(c, n):
        return "x%d_%d" % (c, n)

    idf = carve(0, 128, F32)
    dma("sp", None, xT[:, 0:4, 0:NT], xT_d[:, 0, 0:4], [], [xres(c, 0) for c in range(4)])
    dma("act", None, xT[:, 4:8, 0:NT], xT_d[:, 0, 4:8], [], [xres(c, 0) for c in range(4, 8)])
    for n in range(1, 4):
        dma("sp", None, xT[:, :, n * NT:(n + 1) * NT], xT_d[:, n], [], [xres(c, n) for c in range(DC)])
    dma("act", None, cvt[:], cv_d, [], ["cvt"])
    dma("act", None, xh[:], xh_d, [], ["xh"])
    dma("act", None, idf, id_d, [], ["idf"])
    dve("memset", [], ["ones"], ap=ones[:], constant=1.0)
    dve("memset", [], ["stash"], ap=stash[:], constant=0.0)
    dve("tensor_copy", ["idf"], ["idb"], out=idb[:], in_=idf)

    NRM_LO, NRM_HI = 18480, 23600
    nrm = [NRM_LO]
    sqctr = [0]

    def norm_tile(xs, xr, gname, hd, hr, w):
        sqb = [carve(nrm[0] + 256 * i, 256, BF16) for i in range(2)]
        rs_t = carve(nrm[0] + 512, 512, F32)
        rstd_t = carve(nrm[0] + 1024, 512, F32)
        for c in range(DC):
            i = sqctr[0] % 2
            sqctr[0] += 1
            act([xr[c]], ["sqb%d" % i], out=sqb[i][:, 0:w], in_=xs[c], func=AF.Square)
            pe(["ones", "sqb%d" % i], ["ps5"], out=ps[5][:, 0:w], lhsT=ones[:], rhs=sqb[i][:, 0:w], start=(c == 0), stop=(c == DC - 1))
        act(["ps5"], ["rs_t"], out=rs_t[:, 0:w], in_=ps[5][:, 0:w], func=AF.Sqrt, scale=1.0 / D, bias=EPS)
        dve("reciprocal", ["rs_t"], ["rstd_t"], out=rstd_t[:, 0:w], in_=rs_t[:, 0:w])
        for c in range(DC):
            dve("scalar_tensor_tensor", [xr[c], "rstd_t", "cvt"], [hr[c]], out=hd[c], in0=xs[c], scalar=cvc(gname, c),
                in1=rstd_t[:, 0:w], op0=ALU.mult, op1=ALU.mult)

    exctr = [0]
    exch_ids = set()

    def exchange(ncols):
        i = exctr[0]
        exctr[0] += 1
        assert XCOLS[i] == ncols
        ids = [dma("act", None, ib_d[i].ap(), pay[:, 0:ncols], ["pay"], ["ib%d" % i])]
        ids.append(S.add("pool", _mk("collective_compute", dict(kind="AllGather", op=ALU.bypass, replica_groups=[list(range(NCORES))],
                                                                ins=[ib_d[i].ap().opt()], outs=[ob_d[i].ap().opt()])),
                         ["ib%d" % i], ["ob%d" % i]))
        g3 = gath[:, 0:NCORES * ncols].rearrange("p (r c) -> p r c", r=NCORES)
        ids.append(dma("sp", None, g3, ob_d[i].ap().rearrange("(r p) c -> p r c", p=128), ["ob%d" % i], ["gath"]))
        exch_ids.update(ids)
        return g3

    def masked_sum(dst, dres, g3, col0, ncols, mname, first=True):
        for j in range(NCORES):
            if first and j == 0:
                dve("tensor_scalar", ["gath", "cvt"], [dres], out=dst, in0=g3[:, j, col0:col0 + ncols], scalar1=cvc(mname, j),
                    scalar2=None, op0=ALU.mult)
            else:
                dve("scalar_tensor_tensor", ["gath", "cvt", dres], [dres], out=dst, in0=g3[:, j, col0:col0 + ncols],
                    scalar=cvc(mname, j), in1=dst, op0=ALU.mult, op1=ALU.add)

    def halo_exchange():
        for p in range(2):
            for c in range(DC):
                act([xres(c, 2 * p + 1)], ["pay"], out=pay[:, p * 24 + c * 3:p * 24 + c * 3 + 3],
                    in_=xT[:, c, p * TP + TP - 3:p * TP + TP], func=AF.Copy)
        g3 = exchange(48)

        def consume():
            masked_sum(xh[:, 0:24], "xh", g3, 0, 24, "mP")
            masked_sum(xh[:, 24:48], "xh", g3, 24, 24, "mP")
            masked_sum(xh[:, 24:48], "xh", g3, 0, 24, "mQ", first=False)
        return consume

    hT = carve(0, 4096, BF16, "p (c t) -> p c t", c=DC)
    hh = carve(4096, 16, BF16)
    sqh = carve(4112, 16, BF16)

    def hres(c, n):
        return "hT%d_%d" % (c, n)

    def mixer_norm(L, p, halo, pre_halo=None):
        for n in range(2):
            gn = 2 * p + n
            norm_tile([xT[:, c, gn * NT:(gn + 1) * NT] for c in range(DC)], [xres(c, gn) for c in range(DC)], "gmix%d" % L,
                      [hT[:, c, n * NT:(n + 1) * NT] for c in range(DC)], [hres(c, n) for c in range(DC)], NT)
        if pre_halo is not None:
            pre_halo()
        if halo:
            xhp = xh[:, p * 24:(p + 1) * 24]
            act(["xh"], ["sqh"], out=sqh[:, 0:24], in_=xhp, func=AF.Square)
            for c in range(DC):
                pe(["ones", "sqh"], ["ps6"], out=ps[6][:, 0:3], lhsT=ones[:], rhs=sqh[:, c * 3:c * 3 + 3], start=(c == 0), stop=(c == DC - 1))
            rsh, rstdh = sm("rsh", 3), sm("rstdh", 3)
            act(["ps6"], ["rsh"], out=rsh, in_=ps[6][:, 0:3], func=AF.Sqrt, scale=1.0 / D, bias=EPS)
            dve("reciprocal", ["rsh"], ["rstdh"], out=rstdh, in_=rsh)
            for c in range(DC):
                dve("scalar_tensor_tensor", ["xh", "rstdh", "cvt"], ["hh"], out=hh[:, c * 3:c * 3 + 3], in0=xhp[:, c * 3:c * 3 + 3],
                    scalar=cvc("gmix%d" % L, c), in1=rstdh, op0=ALU.mult, op1=ALU.mult)

    def k8_mm(bank, s, loc, n):
        for k in range(DC):
            pe(["slot%d" % s, hres(k, n)], ["ps%d" % bank], out=ps[bank][:], lhsT=slots[:, s, k * 256 + loc * 128:k * 256 + loc * 128 + 128],
               rhs=hT[:, k, n * NT:(n + 1) * NT], start=(k == 0), stop=(k == DC - 1))

    def k8_halo(col0, s, loc, rname):
        for k in range(DC):
            pe(["slot%d" % s, "hh"], [rname], out=ps[6][:, col0:col0 + 3], lhsT=slots[:, s, k * 256 + loc * 128:k * 256 + loc * 128 + 128],
               rhs=hh[:, k * 3:k * 3 + 3], start=(k == 0), stop=(k == DC - 1))

    def out_proj(widx, ysrc, yres, nk, p, a_style):
        s = None
        for m in range(DC):
            if a_style:
                s, base, stride = wget(widx + m, 1280), 0, 128
            else:
                if m % 2 == 0:
                    s = wget(widx + m // 2)
                base, stride = (m % 2) * 128, 256
            for n in range(2):
                gn = 2 * p + n
                b = mmbank()
                for k in range(nk):
                    pe(["slot%d" % s, yres(k, n)], ["ps%d" % b], out=ps[b][:], lhsT=slots[:, s, k * stride + base:k * stride + base + 128],
                       rhs=ysrc(k, n), start=(k == 0), stop=(k == nk - 1))
                xv_ = xT[:, m, gn * NT:(gn + 1) * NT]
                dve("tensor_tensor", ["ps%d" % b, xres(m, gn)], [xres(m, gn)], out=xv_, in0=ps[b][:], in1=xv_, op=ALU.add)

    Kg, Ggroups, Gplan = _gate_plan()

    def rglru_consts(L):
        j = L // 3
        ev = sm("e")
        lam = cvt[:, cv["alam%d" % j]:cv["alam%d" % j] + 10]
        act(["cvt"], ["ev"], out=ev, in_=lam, func=AF.Exp, scale=-1.0)
        act(["ev"], ["ev"], out=ev, in_=ev, func=AF.Ln, bias=1.0)
        dve("tensor_scalar", ["ev"], ["sc"], out=sm("sc"), in0=ev, scalar1=-8.0, scalar2=None, op0=ALU.mult)
        dve("tensor_scalar", ["ev"], ["sc2"], out=sm("sc2"), in0=ev, scalar1=-4.0, scalar2=None, op0=ALU.mult)
        dve("tensor_scalar", ["cvt"], ["hba"], out=sm("hba"), in0=cvt[:, cv["agab%d" % j]:cv["agab%d" % j] + 10], scalar1=0.5, scalar2=None, op0=ALU.mult)
        dve("tensor_scalar", ["cvt"], ["hbx"], out=sm("hbx"), in0=cvt[:, cv["agxb%d" % j]:cv["agxb%d" % j] + 10], scalar1=0.5, scalar2=None, op0=ALU.mult)

    def rglru(L, p, pre_halo=None):
        j = L // 3
        o = 4128
        xrp = [carve(o + i * 1032, 1032, F32) for i in range(2)]; o += 2 * 1032
        xc = [carve(o + i * 1024, 1024, F32) for i in range(4)]; o += 4 * 1024
        xcb = [carve(o + i * 512, 512, BF16) for i in range(6)]; o += 6 * 512
        hsg = carve(o, 5120, BF16, "p (c t) -> p c t", c=RC); o += 5120
        assert o == NRM_LO, o
        PG = carve(o, 5120, BF16, "p (c t) -> p c t", c=RC); o += 5120
        tR = [carve(o + i * 512, 512, F32) for i in range(2)]; o += 2 * 512
        tI = [carve(o + i * 512, 512, F32) for i in range(2)]; o += 2 * 512
        tA = [carve(o + i * 512, 512, F32) for i in range(2)]; o += 2 * 512
        tH = [carve(o + i * 512, 512, F32) for i in range(2)]; o += 2 * 512
        tP = [carve(o + i * 512, 512, F32) for i in range(2)]; o += 2 * 512
        tG = [carve(o + i * 512, 512, F32) for i in range(2)]; o += 2 * 512
        assert o <= AW, (o, AW)


        def X(c0):
            s = wget(wi["a_x%d" % j] + c0 // 2)
            for loc in range(2):
                c = c0 + loc
                xb = xrp[c % 2]
                for n in range(2):
                    b = mmbank()
                    k8_mm(b, s, loc, n)
                    act(["ps%d" % b], ["xrp%d_%d" % (c % 2, n)], out=xb[:, 3 + n * NT:3 + (n + 1) * NT], in_=ps[b][:], func=AF.Copy)
                hc = (c % 2) * 4
                k8_halo(hc, s, loc, "ps6_%d" % hc)
                act(["ps6_%d" % hc], ["xrp%d_h" % (c % 2)], out=xb[:, 0:3], in_=ps[6][:, hc:hc + 3], func=AF.Copy)
                xr_res = ["xrp%d_0" % (c % 2), "xrp%d_1" % (c % 2), "xrp%d_h" % (c % 2)]
                xo = xc[c % 4]
                xcr = "xc%d" % (c % 4)
                dve("tensor_scalar", xr_res + ["cvt"], [xcr], out=xo[:, :], in0=xb[:, 3:3 + TP], scalar1=cvc("acw%d_3" % j, c),
                    scalar2=cvc("acb%d" % j, c), op0=ALU.mult, op1=ALU.add)
                for kk in (2, 1, 0):
                    dve("scalar_tensor_tensor", xr_res + ["cvt", xcr], [xcr], out=xo[:, :], in0=xb[:, kk:kk + TP],
                        scalar=cvc("acw%d_%d" % (j, kk), c), in1=xo[:, :], op0=ALU.mult, op1=ALU.add)
                dve("tensor_copy", [xcr], ["xcb%d" % (c % 6)], out=xcb[c % 6][:, :], in_=xo[:, :])

        gslot = {}
        gws = [None]

        def G(c):
            gi = [g for g, grp in enumerate(Ggroups) if c in grp][0]
            if gi not in gslot:
                gslot[gi] = wget(wi["a_gt%d" % j] + gi)
            gs = gslot[gi]
            if c % 2 == 0:
                gws[0] = wget(wi["a_g%d" % j] + c // 2)
            ws_ = gws[0]
            xcr = "xc%d" % (c % 4)
            cols = [slice(n * NT, (n + 1) * NT) for n in range(2)]
            bR, bI = [0, 0], [0, 0]
            for n in range(2):
                bR[n], bI[n] = mmbank(), mmbank()
                for gate, bank in ((0, bR[n]), (1, bI[n])):
                    tl = Gplan[(c, gate)][1]
                    for i, (kc, t) in enumerate(tl):
                        pe(["slot%d" % gs, "xcb%d" % (kc % 6)], ["ps%d" % bank], out=ps[bank][:], lhsT=slots[:, gs, t * 128:(t + 1) * 128],
                           rhs=xcb[kc % 6][:, cols[n]], start=(i == 0), stop=(i == len(tl) - 1))
            for n in range(2):
                act(["ps%d" % bR[n], "hba"], ["tR%d" % n], out=tR[n][:, :], in_=ps[bR[n]][:], func=AF.Tanh, scale=0.5, bias=sm("hba", 1, c))
                act(["ps%d" % bI[n], "hbx"], ["tI%d" % n], out=tI[n][:, :], in_=ps[bI[n]][:], func=AF.Tanh, scale=0.5, bias=sm("hbx", 1, c))
            for n in range(2):
                act(["tR%d" % n, "sc2"], ["tA%d" % n], out=tA[n][:, :], in_=tR[n][:, :], func=AF.Exp, scale=sm("sc2", 1, c), bias=sm("sc2", 1, c))
                act(["tR%d" % n, "sc"], ["tR%d" % n], out=tR[n][:, :], in_=tR[n][:, :], func=AF.Exp, scale=sm("sc", 1, c), bias=sm("sc", 1, c))
            for n in range(2):
                act(["tR%d" % n], ["tR%d" % n], out=tR[n][:, :], in_=tR[n][:, :], func=AF.Sqrt, scale=-0.25, bias=0.25)
            bG = [0, 0]
            for n in range(2):
                bG[n] = mmbank()
                k8_mm(bG[n], ws_, c % 2, n)
            for n in range(2):
                dve("scalar_tensor_tensor", ["tR%d" % n, "tI%d" % n], ["tI%d" % n], out=tI[n][:, :], in0=tI[n][:, :], scalar=1.0, in1=tR[n][:, :],
                    op0=ALU.add, op1=ALU.mult)
                dve("tensor_tensor", ["tI%d" % n, xcr], ["tI%d" % n], out=tI[n][:, :], in0=tI[n][:, :], in1=xc[c % 4][:, cols[n]], op=ALU.mult)
                if n == 0:
                    dve("tensor_tensor_scan", ["tA0", "tI0"], ["tH0"], out=tH[0][:, :], data0=tA[0][:, :], data1=tI[0][:, :], initial=0.0,
                        op0=ALU.mult, op1=ALU.add)
                    dve("tensor_tensor_scan", ["tA0"], ["tP0"], out=tP[0][:, :], data0=tA[0][:, :], data1=tA[0][:, :], initial=1.0,
                        op0=ALU.mult, op1=ALU.min)
                else:
                    dve("tensor_tensor_scan", ["tA1", "tI1", "tH0"], ["tH1"], out=tH[1][:, :], data0=tA[1][:, :], data1=tI[1][:, :],
                        initial=tH[0][:, NT - 1:NT], op0=ALU.mult, op1=ALU.add)
                    dve("tensor_tensor_scan", ["tA1", "tP0"], ["tP1"], out=tP[1][:, :], data0=tA[1][:, :], data1=tA[1][:, :],
                        initial=tP[0][:, NT - 1:NT], op0=ALU.mult, op1=ALU.min)
            for n in range(2):
                act(["ps%d" % bG[n]], ["tG%d" % n], out=tG[n][:, :], in_=ps[bG[n]][:], func=AF.Gelu_apprx_tanh)
            for n in range(2):
                dve("tensor_tensor", ["tH%d" % n, "tG%d" % n], ["hsg%d_%d" % (c, n)], out=hsg[:, c, cols[n]], in0=tH[n][:, :], in1=tG[n][:, :], op=ALU.mult)
                dve("tensor_tensor", ["tP%d" % n, "tG%d" % n], ["PG%d" % c], out=PG[:, c, cols[n]], in0=tP[n][:, :], in1=tG[n][:, :], op=ALU.mult)
            dve("tensor_copy", ["tP1"], ["pay"], out=pay[:, c:c + 1], in_=tP[1][:, NT - 1:NT])
            dve("tensor_copy", ["tH1"], ["pay"], out=pay[:, 10 + c:11 + c], in_=tH[1][:, NT - 1:NT])

        st = {}

        def front():
            nrm[0] = NRM_HI
            mixer_norm(L, p, True, pre_halo)
            X(0)
            X(2)

        def rest():
            mmpool[0] = [0, 1, 2, 3, 4, 7, 5]
            for step in ("G0", "G1", "X4", "G2", "G3", "X6", "G4", "G5", "X8", "G6", "G7", "G8", "G9"):
                (X if step[0] == "X" else G)(int(step[1:]))
            mmpool[0] = [0, 1, 2, 3, 4, 7]
            dve("tensor_copy", ["stash"], ["pay"], out=pay[:, 20:30], in_=stash[:, :])
            st["g3"] = exchange(30)

        def tail():
            _rg_tail(L, p, dict(hsg=hsg, PG=PG), st["g3"])

        return front, rest, tail

    def _rg_tail(L, p, env, g3):
        j = L // 3
        hsg, PG = env["hsg"], env["PG"]
        Hc, t1, t2 = sm("Hc"), sm("t1"), sm("t2")
        if p == 0:
            dve("memset", [], ["Hc"], ap=Hc, constant=0.0)
        else:
            masked_sum(Hc, "Hc", g3, 20, 10, "mS")
        for r in range(NCORES):
            dve("tensor_scalar", ["gath", "cvt"], ["t1"], out=t1, in0=g3[:, r, 0:10], scalar1=-1.0, scalar2=cvc("mC", r),
                op0=ALU.add, op1=ALU.mult)
            dve("scalar_tensor_tensor", ["t1", "Hc"], ["t2"], out=t2, in0=t1, scalar=1.0, in1=Hc, op0=ALU.add, op1=ALU.mult)
            dve("scalar_tensor_tensor", ["gath", "cvt", "t2"], ["Hc"], out=Hc, in0=g3[:, r, 10:20], scalar=cvc("mC", r), in1=t2,
                op0=ALU.mult, op1=ALU.add)
        dve("tensor_tensor", ["pay", "Hc"], ["t1"], out=t1, in0=pay[:, 0:10], in1=Hc, op=ALU.mult)
        dve("tensor_tensor", ["t1", "pay"], ["stash"], out=stash[:, :], in0=t1, in1=pay[:, 10:20], op=ALU.add)
        for n in range(2):
            for c in range(RC):
                dve("scalar_tensor_tensor", ["PG%d" % c, "Hc", "hsg%d_%d" % (c, n)], ["hsg%d_%d" % (c, n)],
                    out=hsg[:, c, n * NT:(n + 1) * NT], in0=PG[:, c, n * NT:(n + 1) * NT], scalar=sm("Hc", 1, c),
                    in1=hsg[:, c, n * NT:(n + 1) * NT], op0=ALU.mult, op1=ALU.add)
        out_proj(wi["a_o%d" % j], lambda k, n: hsg[:, k, n * NT:(n + 1) * NT], lambda k, n: "hsg%d_%d" % (k, n), RC, p, True)

    def sgu(L, p, wsm, bb):
        o = 4128
        v = carve(o, 8192, F32, "p (c t) -> p c t", c=DC); o += 8192
        vn = carve(o, 4096, BF16, "p (c t) -> p c t", c=DC); o += 4096
        vT = carve(o, 4096, BF16, "p (a c) -> p a c", a=8); o += 4096
        y = carve(o, 4096, BF16, "p (c t) -> p c t", c=DC); o += 4096
        ut = [carve(o + i * 512, 512, F32) for i in range(2)]; o += 1024
        mt = [carve(o + i * 512, 512, F32) for i in range(2)]; o += 1024
        bb = carve(o, 1024, F32); o += 1024
        assert o <= AW
        psT = ps[7][:].bitcast(BF16)
        NTMP = ["rstd_t", "rs_t", "sqb0", "sqb1"]

        def front():
            nrm[0] = NRM_LO
            if p == 0:
                dma("act", None, bb, sb_d, [], ["bb"])
            mixer_norm(L, p, False)

        def outp():
            out_proj(wi["b_o"], lambda k, n: y[:, k, n * NT:(n + 1) * NT], lambda k, n: "y%d_%d" % (k, n), DC, p, False)

        def body():
            _sgu_body(L, p, wsm, dict(v=v, vn=vn, vT=vT, y=y, ut=ut, mt=mt, bb=bb, psT=psT, NTMP=NTMP))

        return front, body, outp

    def _sgu_body(L, p, wsm, env):
        v, vn, vT, y, ut, mt, bb, psT, NTMP = [env[k] for k in ("v", "vn", "vT", "y", "ut", "mt", "bb", "psT", "NTMP")]
        if True:
            nrm[0] = NRM_LO
        s = None
        for vc in range(DC):
            if vc % 2 == 0:
                s = wget(wi["b_v"] + vc // 2)
            for n in range(2):
                b = mmbank()
                k8_mm(b, s, vc % 2, n)
                act(["ps%d" % b], ["v%d_%d" % (vc, n)], out=v[:, vc, n * NT:(n + 1) * NT], in_=ps[b][:], func=AF.Gelu_apprx_tanh)
        for n in range(2):
            norm_tile([v[:, c, n * NT:(n + 1) * NT] for c in range(DC)], ["v%d_%d" % (c, n) for c in range(DC)], "bng",
                      [vn[:, c, n * NT:(n + 1) * NT] for c in range(DC)], ["vn%d_%d" % (c, n) for c in range(DC)], NT)
        for tch in range(8):
            for g in range(8):
                pet(["vn%d_%d" % (g, tch // 4), "idb"] + NTMP, ["ps7"], out=psT[:, g * 128:(g + 1) * 128],
                    in_=vn[:, g, tch * 128:(tch + 1) * 128], identity=idb[:])
            act(["ps7"] + NTMP, ["vT%d" % tch] + NTMP, out=vT[:, tch, :], in_=psT[:, :], func=AF.Copy)
        for g in range(8):
            if g % 2 == 0:
                s = wget(wi["b_u"] + g // 2)
            for n in range(2):
                b = mmbank()
                k8_mm(b, s, g % 2, n)
                act(["ps%d" % b], ["ut%d" % n], out=ut[n][:, :], in_=ps[b][:], func=AF.Gelu_apprx_tanh)
                bm = mmbank()
                for tl in range(4):
                    tch = 4 * n + tl
                    pe(["vT%d" % tch, "wsm"] + NTMP, ["ps%d" % bm], out=ps[bm][:, tl * 128:(tl + 1) * 128], lhsT=vT[:, tch, g * 128:(g + 1) * 128],
                       rhs=wsm[:, g * 128:(g + 1) * 128], start=True, stop=True)
                dve("tensor_tensor", ["ps%d" % bm, "bb"], ["mt%d" % n], out=mt[n][:, :].rearrange("p (a t) -> p a t", a=4),
                    in0=ps[bm][:].rearrange("p (a t) -> p a t", a=4),
                    in1=bb[:, g * 128:(g + 1) * 128].unsqueeze(1).broadcast_to([128, 4, 128]), op=ALU.add)
                dve("tensor_tensor", ["mt%d" % n, "ut%d" % n], ["y%d_%d" % (g, n)], out=y[:, g, n * NT:(n + 1) * NT], in0=mt[n][:, :], in1=ut[n][:, :], op=ALU.mult)

    def sconv(L, p, pre_halo=None):
        o = 4128
        pb = [carve(o + i * 1032, 1032, F32) for i in range(2)]; o += 2 * 1032
        gbs = [carve(o + i * 1024, 1024, F32) for i in range(2)]; o += 2 * 1024
        qt = [carve(o + i * 1024, 1024, F32) for i in range(2)]; o += 2 * 1024
        xvs = [carve(o + i * 512, 512, F32) for i in range(2)]; o += 1024
        y = carve(o, 4096, BF16, "p (c t) -> p c t", c=DC); o += 4096
        assert o <= NRM_LO

        def front():
            nrm[0] = NRM_LO
            mixer_norm(L, p, True, pre_halo)

        def outp():
            out_proj(wi["c_o"], lambda k, n: y[:, k, n * NT:(n + 1) * NT], lambda k, n: "y%d_%d" % (k, n), DC, p, False)

        def body():
            _sconv_body(L, p, dict(pb=pb, gbs=gbs, qt=qt, xvs=xvs, y=y))

        return front, body, outp

    def _sconv_body(L, p, env):
        pb, gbs, qt, xvs, y = [env[k] for k in ("pb", "gbs", "qt", "xvs", "y")]
        sl = {}
        for c in range(DC):
            pbuf = pb[c % 2]
            for q in (3 * c, 3 * c + 1, 3 * c + 2):
                if q // 2 not in sl:
                    sl[q // 2] = wget(wi["c_i"] + q // 2)
            sgc, sxv, sgb = sl[(3 * c) // 2], sl[(3 * c + 1) // 2], sl[(3 * c + 2) // 2]
            lgc, lxv, lgb = (3 * c) % 2, (3 * c + 1) % 2, (3 * c + 2) % 2
            for n in range(2):
                bxv, bgc, bgb = mmbank(), mmbank(), mmbank()
                k8_mm(bxv, sxv, lxv, n)
                k8_mm(bgc, sgc, lgc, n)
                k8_mm(bgb, sgb, lgb, n)
                act(["ps%d" % bxv], ["xvs%d" % n], out=xvs[n][:, :], in_=ps[bxv][:], func=AF.Copy)
                dve("tensor_tensor", ["ps%d" % bgc, "xvs%d" % n], ["pb%d_%d" % (c % 2, n)], out=pbuf[:, 3 + n * NT:3 + (n + 1) * NT],
                    in0=ps[bgc][:], in1=xvs[n][:, :], op=ALU.mult)
                act(["ps%d" % bgb], ["gbs%d_%d" % (c % 2, n)], out=gbs[c % 2][:, n * NT:(n + 1) * NT], in_=ps[bgb][:], func=AF.Copy)
            hc = (c % 2) * 8
            t3 = sm("tmp3", 3, (c % 2) * 4)
            k8_halo(hc, sxv, lxv, "ps6_%d" % hc)
            act(["ps6_%d" % hc], ["t3_%d" % (c % 2)], out=t3, in_=ps[6][:, hc:hc + 3], func=AF.Copy)
            k8_halo(hc + 4, sgc, lgc, "ps6_%d" % (hc + 4))
            dve("tensor_tensor", ["ps6_%d" % (hc + 4), "t3_%d" % (c % 2)], ["pb%d_h" % (c % 2)], out=pbuf[:, 0:3], in0=ps[6][:, hc + 4:hc + 7], in1=t3, op=ALU.mult)
            pres = ["pb%d_0" % (c % 2), "pb%d_1" % (c % 2), "pb%d_h" % (c % 2)]
            qq = qt[c % 2]
            qr = "qt%d" % (c % 2)
            act(pres + ["cvt"], [qr], out=qq[:, :], in_=pbuf[:, 3:3 + TP], func=AF.Copy, scale=cvc("ccw2", c))
            for kk, off in ((1, 2), (0, 1)):
                dve("scalar_tensor_tensor", pres + ["cvt", qr], [qr], out=qq[:, :], in0=pbuf[:, off:off + TP], scalar=cvc("ccw%d" % kk, c),
                    in1=qq[:, :], op0=ALU.mult, op1=ALU.add)
            dve("tensor_tensor", ["gbs%d_0" % (c % 2), "gbs%d_1" % (c % 2), qr], ["y%d_0" % c, "y%d_1" % c], out=y[:, c, :], in0=gbs[c % 2][:, :],
                in1=qq[:, :], op=ALU.mult)

    hF = carve(0, 8192, BF16, "p (c t) -> p c t", c=DC)
    hid = [carve(8192 + i * 1024, 1024, BF16, "p (c t) -> p c t", c=4) for i in range(2)]
    rt = [carve(8192 + 2048 + i * 512, 512, F32) for i in range(2)]

    def mlp(L, hook=None):
        nrm[0] = NRM_HI

        def nt(n):
            norm_tile([xT[:, c, n * NT:(n + 1) * NT] for c in range(DC)], [xres(c, n) for c in range(DC)], "gmlp%d" % L,
                      [hF[:, c, n * NT:(n + 1) * NT] for c in range(DC)], ["hF%d_%d" % (c, n) for c in range(DC)], NT)
        nt(0)
        nt(1)
        if hook is None:
            nt(2)
            nt(3)
        rctr = [0]
        for jg in range(8):
            s1 = [wget(wi["m1_%d" % L] + 2 * jg + i) for i in range(2)]
            s2 = [wget(wi["m2_%d" % L] + 2 * jg + i) for i in range(2)]

            def Hd(n):
                for jl in range(4):
                    b = mmbank()
                    s = s1[jl // 2]
                    for k in range(DC):
                        pe(["slot%d" % s, "hF%d_%d" % (k, n)], ["ps%d" % b], out=ps[b][:],
                           lhsT=slots[:, s, k * 256 + (jl % 2) * 128:k * 256 + (jl % 2) * 128 + 128],
                           rhs=hF[:, k, n * NT:(n + 1) * NT], start=(k == 0), stop=(k == DC - 1))
                    ri = rctr[0] % 2
                    rctr[0] += 1
                    act(["ps%d" % b], ["rt%d" % ri], out=rt[ri][:, :], in_=ps[b][:], func=AF.Relu)
                    dve("tensor_tensor", ["rt%d" % ri], ["hid%d_%d" % (n % 2, jl)], out=hid[n % 2][:, jl, :], in0=rt[ri][:, :], in1=rt[ri][:, :], op=ALU.mult)

            def Od(n):
                for m in range(DC):
                    b = mmbank()
                    s = s2[m // 4]
                    for k in range(4):
                        pe(["slot%d" % s, "hid%d_%d" % (n % 2, k)], ["ps%d" % b], out=ps[b][:],
                           lhsT=slots[:, s, k * 512 + (m % 4) * 128:k * 512 + (m % 4) * 128 + 128],
                           rhs=hid[n % 2][:, k, :], start=(k == 0), stop=(k == 3))
                    xv_ = xT[:, m, n * NT:(n + 1) * NT]
                    dve("tensor_tensor", ["ps%d" % b, xres(m, n)], [xres(m, n)], out=xv_, in0=ps[b][:], in1=xv_, op=ALU.add)

            if jg == 0 and hook is not None:
                Hd(0)
                Hd(1)
                hook()
                s1[:] = [wget(wi["m1_%d" % L] + 2 * jg + i) for i in range(2)]
                s2[:] = [wget(wi["m2_%d" % L] + 2 * jg + i) for i in range(2)]
                Od(0)
                Od(1)
                nrm[0] = NRM_HI
                nt(2)
                nt(3)
                Hd(2)
                Hd(3)
                Od(2)
                Od(3)
                continue
            Hd(0)
            for n in range(4):
                if n < 3:
                    Hd(n + 1)
                Od(n)

    for L in range(nl):
        kind = L % 3
        if kind == 0:
            consume = halo_exchange() if L > 0 else None
            rglru_consts(L)
            if L > 0:
                S.barrier(exclude=exch_ids)
            f0, r0, t0 = rglru(L, 0, consume)
            f1, r1, t1 = rglru(L, 1, None)
            f0()
            r0()
            S.barrier(exclude=exch_ids)
            f1()
            t0()
            r1()
            S.barrier(exclude=exch_ids)
            mlp(L, hook=t1)
        elif kind == 1:
            wsf = carve(12288, 1024, F32)
            trif = carve(13312, 128, F32)
            dma("act", None, wsf, wsT_d, [], ["wsf"])
            dma("act", None, trif, tri_d, [], ["trif"])
            dve("tensor_tensor", ["wsf", "trif"], ["wsm"], out=wsmt[:].rearrange("p (g t) -> p g t", g=8),
                in0=wsf.rearrange("p (g t) -> p g t", g=8), in1=trif.unsqueeze(1).broadcast_to([128, 8, 128]), op=ALU.mult)
            S.barrier(exclude=exch_ids)
            mmpool[0] = [0, 1, 2, 3, 4]
            (f0, b0, o0), (f1, b1, o1) = sgu(L, 0, wsmt, None), sgu(L, 1, wsmt, None)
            f0(); b0(); f1(); o0(); b1(); o1()
            mmpool[0] = [0, 1, 2, 3, 4, 7]
            S.barrier(exclude=exch_ids)
            mlp(L)
        else:
            consume = halo_exchange()
            S.barrier(exclude=exch_ids)
            (f0, b0, o0), (f1, b1, o1) = sconv(L, 0, consume), sconv(L, 1, None)
            f0(); b0(); f1(); o0(); b1(); o1()
            S.barrier(exclude=exch_ids)
            mlp(L)
    S.barrier()

    finals = []
    nrm[0] = NRM_HI
    ot = [carve(i * 4096, 4096, F32, "p (c t) -> p c t", c=DC) for i in range(2)]
    for n in range(4):
        o_t = ot[n % 2]
        ores = ["ot%d_%d" % (n % 2, c) for c in range(DC)]
        if final_norm:
            norm_tile([xT[:, c, n * NT:(n + 1) * NT] for c in range(DC)], [xres(c, n) for c in range(DC)], "gfin",
                      [o_t[:, c, :] for c in range(DC)], ores, NT)
        else:
            for c in range(DC):
                act([xres(c, n)], [ores[c]], out=o_t[:, c, :], in_=xT[:, c, n * NT:(n + 1) * NT], func=AF.Copy)
        finals.append(dma("sp", None, out_d[:, n], o_t, ores, ["outd%d" % n]))

    S.emit(nc, dict(pe="tensor", act="scalar", dve="vector", pool="gpsimd", sp="sync"), finals)
    es.close()
    return nc, cv, NCV, wi, NSL


def _prep(inputs):
    f = lambda k: np.asarray(inputs[k], np.float32)
    x = f("x")
    _, cv, NCV, wi, NSL = _layout_cache()
    wall = np.zeros((NSL, 128, SL), np.float32)
    for j in range(2):
        W = f("a_w_in")[j]
        wall[wi["a_g%d" % j]:wi["a_g%d" % j] + 5] = _k8(W, [c * 128 for c in range(10)])
        wall[wi["a_x%d" % j]:wi["a_x%d" % j] + 5] = _k8(W, [DR + c * 128 for c in range(10)])
        wall[wi["a_gt%d" % j]:wi["a_gt%d" % j] + 4] = _gates(f("a_gate_a_w")[j], f("a_gate_x_w")[j])
        wall[wi["a_o%d" % j]:wi["a_o%d" % j] + 8] = _aout(f("a_w_out")[j])
    Wb = f("b_w_in")[0]
    wall[wi["b_u"]:wi["b_u"] + 4] = _k8(Wb, [c * 128 for c in range(8)])
    wall[wi["b_v"]:wi["b_v"] + 4] = _k8(Wb, [D + c * 128 for c in range(8)])
    wall[wi["b_o"]:wi["b_o"] + 4] = _k8(f("b_w_out")[0], [c * 128 for c in range(8)])
    Wc = f("c_w_in")[0]
    cols = []
    for c in range(8):
        cols += [D + c * 128, 2 * D + c * 128, c * 128]
    wall[wi["c_i"]:wi["c_i"] + 12] = _k8(Wc, cols)
    wall[wi["c_o"]:wi["c_o"] + 4] = _k8(f("c_w_out")[0], [c * 128 for c in range(8)])
    for i in range(4):
        wall[wi["m1_%d" % i]:wi["m1_%d" % i] + 16] = _k8(f("mlp_w1")[i], [c * 128 for c in range(32)])
        wall[wi["m2_%d" % i]:wi["m2_%d" % i] + 16] = _w2(f("mlp_w2")[i])

    cvec = np.zeros((128, NCV), np.float32)

    def put(name, arr):
        cvec[:, cv[name]:cv[name] + arr.shape[1]] = arr
    for i in range(4):
        put("gmix%d" % i, _vec(f("norm_mix_g")[i], 8))
        put("gmlp%d" % i, _vec(f("norm_mlp_g")[i], 8))
    put("gfin", _vec(f("final_norm_g"), 8))
    for j in range(2):
        for k in range(4):
            put("acw%d_%d" % (j, k), _vec(f("a_conv_w")[j, k], 10))
        put("acb%d" % j, _vec(f("a_conv_b")[j], 10))
        put("agab%d" % j, _vec(f("a_gate_a_b")[j], 10))
        put("agxb%d" % j, _vec(f("a_gate_x_b")[j], 10))
        put("alam%d" % j, _vec(f("a_lambda")[j], 10))
    put("bng", _vec(f("b_norm_g")[0], 8))
    for k in range(3):
        put("ccw%d" % k, _vec(f("c_conv_w")[0, k], 8))

    wsT = np.ascontiguousarray(f("b_w_s")[0].transpose(2, 0, 1)).reshape(128, 1024)
    tri = np.triu(np.ones((128, 128), np.float32))
    sbias = np.ascontiguousarray(np.broadcast_to(f("b_s_bias")[0].reshape(1, 1024), (128, 1024)))
    ident = np.eye(128, dtype=np.float32)

    in_maps = []
    for k in range(NCORES):
        b, r = k // 4, k % 4
        segs = [(r * TP, (r + 1) * TP), (4096 + r * TP, 4096 + (r + 1) * TP)]
        xs = np.concatenate([x[b, s0:s1] for s0, s1 in segs], axis=0)
        xT = np.ascontiguousarray(xs.T.reshape(DC, 128, 4, NT).transpose(1, 2, 0, 3))
        xh = np.zeros((128, 2, DC, 3), np.float32)
        for p, (s0, _) in enumerate(segs):
            if s0 > 0:
                xh[:, p] = x[b, s0 - 3:s0].T.reshape(DC, 128, 3).transpose(1, 0, 2)
        cvk = cvec.copy()
        m = np.zeros((4, NCORES), np.float32)
        if r > 0:
            m[0, k - 1] = 1.0
        else:
            m[1, b * 4 + 3] = 1.0
        for jj in range(b * 4, k):
            m[2, jj] = 1.0
        m[3, b * 4 + 3] = 1.0
        for i, nm in enumerate(("mP", "mQ", "mC", "mS")):
            cvk[:, cv[nm]:cv[nm] + 8] = m[i][None, :]
        in_maps.append(dict(xT=xT, xhalo=np.ascontiguousarray(xh.reshape(128, 48)), cvec=cvk, wall=wall, wsT=wsT, tri=tri,
                            sbias=sbias, ident=ident))
    return in_maps


_CACHE = {}


def _layout_cache():
    if "nc" not in _CACHE:
        _CACHE["nc"] = build()
    return _CACHE["nc"]


def _assemble(results):
    out = np.zeros((2, SEQ, D), np.float32)
    for k in range(NCORES):
        b, r = k // 4, k % 4
        oT = results[k]["outT"]
        o = oT.transpose(1, 3, 2, 0).reshape(TOK, D)
        out[b, r * TP:(r + 1) * TP] = o[:TP]
        out[b, 4096 + r * TP:4096 + (r + 1) * TP] = o[TP:]
    return out


def kernel(**inputs):
    nc = _layout_cache()[0]
    in_maps = _prep(inputs)
    res = run_bass_kernel_spmd(nc, in_maps, core_ids=list(range(NCORES)))
    return _assemble(res.results)
```
